# Optimizing a Trainium2 kernel written in Bass

```python
import math
import jax, jax.numpy as jnp
from jax import lax
import numpy as np

D_MODEL = 1024
BATCH = 4
SEQ = 8192
DEPTH = 4

HEAD_DIM = 64
SWA_HEADS = 8
SWA_KV_HEADS = 2
SWA_WINDOW = 128
SWA_BLOCK = 128
RNN_WIDTH = D_MODEL
RNN_BLOCKS = 16
RNN_BLOCK_WIDTH = RNN_WIDTH // RNN_BLOCKS
RNN_CONV = 4
RGLRU_C = 8.0
MOBA_HEADS = 8
MOBA_BLOCK = 256
MOBA_TOPK = 3
NUM_BUCKETS = 32
MAX_DISTANCE = 2048
D_FF = 2816
FFN_CONV = 3

RMS_EPS = 1e-6
NEG_INF = -1e30
N_BRANCH = 3
SWA_Q = SWA_HEADS * HEAD_DIM
SWA_KV = SWA_KV_HEADS * HEAD_DIM
MOBA_W = MOBA_HEADS * HEAD_DIM
IN_WIDTHS = (SWA_Q, SWA_KV, SWA_KV, RNN_WIDTH, RNN_WIDTH, MOBA_W, MOBA_W, MOBA_W, N_BRANCH * D_MODEL)
IN_OFFSETS = tuple(sum(IN_WIDTHS[:i + 1]) for i in range(len(IN_WIDTHS) - 1))
D_IN = sum(IN_WIDTHS)
D_MIX = SWA_Q + RNN_WIDTH + MOBA_W

kernel_name = "hybrid_swa_rglru_moba_convffn_trunk"


def rms_norm(x, gain):
    xf = x.astype(jnp.float32)
    y = xf * lax.rsqrt(jnp.mean(xf * xf, axis=-1, keepdims=True) + RMS_EPS)
    return (y * gain.astype(jnp.float32)).astype(x.dtype)


def t5_bucket(dist):
    dist = jnp.maximum(dist, 0)
    max_exact = NUM_BUCKETS // 2
    log_ratio = jnp.log(jnp.maximum(dist, 1).astype(jnp.float32) / max_exact) / math.log(MAX_DISTANCE / max_exact)
    large = max_exact + (log_ratio * (NUM_BUCKETS - max_exact)).astype(jnp.int32)
    large = jnp.minimum(large, NUM_BUCKETS - 1)
    return jnp.where(dist < max_exact, dist, large)


def causal_dwconv(x, w, b):
    k_width = w.shape[0]
    s = x.shape[1]
    xp = jnp.pad(x, ((0, 0), (k_width - 1, 0), (0, 0)))
    return sum(xp[:, k:k + s] * w[k] for k in range(k_width)) + b


def swa_attention(q, k, v, sinks, bias_a):
    b_sz, s_len = q.shape[:2]
    L = SWA_BLOCK
    nblk = s_len // L
    grp = SWA_HEADS // SWA_KV_HEADS
    qb = q.reshape(b_sz, nblk, L, SWA_KV_HEADS, grp, HEAD_DIM)

    def band(t):
        tp = jnp.pad(t, ((0, 0), (L, 0), (0, 0), (0, 0))).reshape(b_sz, nblk + 1, L, SWA_KV_HEADS, HEAD_DIM)
        return jnp.concatenate([tp[:, :-1], tp[:, 1:]], axis=2)

    kb, vb = band(k), band(v)
    s = jnp.einsum('bnqkgd,bnskd->bnkgqs', qb, kb).astype(jnp.float32) * (HEAD_DIM ** -0.5)
    qi = jnp.arange(L)
    sj = jnp.arange(2 * L)
    diff = qi[:, None] + L - sj[None, :]
    bias = bias_a[t5_bucket(diff)].astype(jnp.float32).transpose(2, 0, 1).reshape(SWA_KV_HEADS, grp, L, 2 * L)
    key_pos = jnp.arange(nblk)[:, None] * L - L + sj[None, :]
    valid = ((diff >= 0) & (diff < SWA_WINDOW))[None] & (key_pos >= 0)[:, None, :]
    s = jnp.where(valid[None, :, None, None], s + bias, NEG_INF)
    sink = sinks.astype(jnp.float32).reshape(SWA_KV_HEADS, grp)[:, :, None, None]
    m = jnp.maximum(jnp.max(s, axis=-1, keepdims=True), sink)
    p = jnp.exp(s - m)
    p = p / (jnp.sum(p, axis=-1, keepdims=True) + jnp.exp(sink - m))
    o = jnp.einsum('bnkgqs,bnskd->bnqkgd', p.astype(v.dtype), vb)
    return o.reshape(b_sz, s_len, SWA_Q)


def rg_lru(x, ga_w, ga_b, gx_w, gx_b, lam):
    b_sz, s_len, width = x.shape
    xb = x.reshape(b_sz, s_len, RNN_BLOCKS, RNN_BLOCK_WIDTH)
    r = jax.nn.sigmoid(jnp.einsum('bsnc,ncd->bsnd', xb, ga_w).reshape(b_sz, s_len, width) + ga_b)
    i = jax.nn.sigmoid(jnp.einsum('bsnc,ncd->bsnd', xb, gx_w).reshape(b_sz, s_len, width) + gx_b)
    log_a = -RGLRU_C * r.astype(jnp.float32) * jax.nn.softplus(-lam.astype(jnp.float32))
    a = jnp.exp(log_a)
    mult = jnp.sqrt(-jnp.expm1(2.0 * log_a))
    mult = jnp.where((jnp.arange(s_len) == 0)[None, :, None], 1.0, mult)
    u = mult * (i * x).astype(jnp.float32)

    def combine(left, right):
        a_l, b_l = left
        a_r, b_r = right
        return (a_l * a_r, a_r * b_l + b_r)

    _, h = lax.associative_scan(combine, (a, u), axis=1)
    return h.astype(x.dtype)


def moba_attention(q, k, v, bias_c):
    b_sz, s_len = q.shape[:2]
    MB = MOBA_BLOCK
    nb = -(-s_len // MB)
    s_pad = nb * MB
    pad = ((0, 0), (0, s_pad - s_len), (0, 0), (0, 0))
    q, k, v = jnp.pad(q, pad), jnp.pad(k, pad), jnp.pad(v, pad)
    topk = min(MOBA_TOPK, nb)
    kb = k.reshape(b_sz, nb, MB, MOBA_HEADS, HEAD_DIM)
    vb = v.reshape(b_sz, nb, MB, MOBA_HEADS, HEAD_DIM)
    k_mean = jnp.mean(kb.astype(jnp.float32), axis=2).astype(k.dtype)
    k_bh = kb.transpose(0, 3, 1, 2, 4)
    v_bh = vb.transpose(0, 3, 1, 2, 4)
    q_chunks = q.reshape(b_sz, nb, MB, MOBA_HEADS, HEAD_DIM).transpose(1, 0, 2, 3, 4)
    k_own = kb.transpose(1, 0, 2, 3, 4)
    v_own = vb.transpose(1, 0, 2, 3, 4)
    pos = jnp.arange(MB)
    own_diff = pos[:, None] - pos[None, :]
    own_mask = own_diff >= 0
    own_bias = bias_c[t5_bucket(own_diff)].astype(jnp.float32).transpose(0, 2, 1)
    b_ix = jnp.arange(b_sz)[:, None, None, None]
    h_ix = jnp.arange(MOBA_HEADS)[None, None, :, None]
    blk_ids = jnp.arange(nb)
    scale = HEAD_DIM ** -0.5

    def one_block(args):
        n, q_c, k_c, v_c = args
        gate = jnp.einsum('bqhd,bmhd->bqhm', q_c, k_mean).astype(jnp.float32)
        gate = jnp.where(blk_ids < n, gate, NEG_INF)
        _, sel = lax.top_k(gate, topk)
        sel_valid = sel < n
        k_sel = k_bh[b_ix, h_ix, sel]
        v_sel = v_bh[b_ix, h_ix, sel]
        s_sel = jnp.einsum('bqhd,bqhtsd->bqhts', q_c, k_sel).astype(jnp.float32) * scale
        dist = (n * MB + pos)[None, :, None, None, None] - (sel[..., None] * MB + pos)
        s_sel = s_sel + bias_c[t5_bucket(dist), h_ix[..., None]].astype(jnp.float32)
        s_sel = jnp.where(sel_valid[..., None], s_sel, NEG_INF).reshape(b_sz, MB, MOBA_HEADS, topk * MB)
        s_own = jnp.einsum('bqhd,bshd->bqhs', q_c, k_c).astype(jnp.float32) * scale + own_bias
        s_own = jnp.where(own_mask[:, None, :], s_own, NEG_INF)
        p = jax.nn.softmax(jnp.concatenate([s_sel, s_own], axis=-1), axis=-1).astype(v.dtype)
        p_sel = p[..., :topk * MB].reshape(b_sz, MB, MOBA_HEADS, topk, MB)
        p_own = p[..., topk * MB:]
        return jnp.einsum('bqhts,bqhtsd->bqhd', p_sel, v_sel) + jnp.einsum('bqhs,bshd->bqhd', p_own, v_c)

    o = lax.map(one_block, (blk_ids, q_chunks, k_own, v_own))
    return o.transpose(1, 0, 2, 3, 4).reshape(b_sz, s_pad, MOBA_W)[:, :s_len]


def setup_inputs(seed: int = 0) -> dict:
    key = jax.random.key(seed)
    ks = jax.random.split(key, 32)
    f32 = jnp.float32

    def nrm(k, shape, fan_in, scale=1.0):
        return jax.random.normal(k, shape, f32) * (scale * fan_in ** -0.5)

    def gain(k, shape):
        return 1.0 + 0.05 * jax.random.normal(k, shape, f32)

    def small(k, shape, scale=0.02):
        return scale * jax.random.normal(k, shape, f32)

    a_init = jax.random.uniform(ks[16], (DEPTH, RNN_WIDTH), f32, 0.9, 0.999)
    w_branch = jnp.concatenate([
        nrm(ks[20], (DEPTH, SWA_Q, D_MODEL), SWA_Q),
        nrm(ks[21], (DEPTH, RNN_WIDTH, D_MODEL), RNN_WIDTH),
        nrm(ks[22], (DEPTH, MOBA_W, D_MODEL), MOBA_W)], axis=1)
    return {
        'x': jax.random.normal(ks[0], (BATCH, SEQ, D_MODEL), f32),
        'c': jax.random.normal(ks[1], (BATCH, D_MODEL), f32),
        'w_mod': nrm(ks[2], (DEPTH, D_MODEL, 6 * D_MODEL), D_MODEL, 0.5),
        'b_mod': small(ks[3], (DEPTH, 6 * D_MODEL)),
        'norm_mix': gain(ks[4], (DEPTH, D_MODEL)),
        'norm_ffn': gain(ks[5], (DEPTH, D_MODEL)),
        'w_in': nrm(ks[6], (DEPTH, D_MODEL, D_IN), D_MODEL),
        'qnorm_a': gain(ks[7], (DEPTH, HEAD_DIM)),
        'knorm_a': gain(ks[8], (DEPTH, HEAD_DIM)),
        'sinks': small(ks[9], (DEPTH, SWA_HEADS), 0.5),
        'rnn_conv_w': nrm(ks[10], (DEPTH, RNN_CONV, RNN_WIDTH), RNN_CONV),
        'rnn_conv_b': small(ks[11], (DEPTH, RNN_WIDTH)),
        'rnn_gate_a_w': nrm(ks[12], (DEPTH, RNN_BLOCKS, RNN_BLOCK_WIDTH, RNN_BLOCK_WIDTH), RNN_BLOCK_WIDTH),
        'rnn_gate_a_b': small(ks[13], (DEPTH, RNN_WIDTH)),
        'rnn_gate_x_w': nrm(ks[14], (DEPTH, RNN_BLOCKS, RNN_BLOCK_WIDTH, RNN_BLOCK_WIDTH), RNN_BLOCK_WIDTH),
        'rnn_gate_x_b': small(ks[15], (DEPTH, RNN_WIDTH)),
        'rnn_lambda': jnp.log(a_init) - jnp.log1p(-a_init),
        'qnorm_c': gain(ks[17], (DEPTH, HEAD_DIM)),
        'knorm_c': gain(ks[18], (DEPTH, HEAD_DIM)),
        'rel_bias': small(ks[19], (NUM_BUCKETS, SWA_HEADS + MOBA_HEADS), 0.5),
        'w_branch': w_branch,
        'w_out': nrm(ks[23], (DEPTH, D_MODEL, D_MODEL), D_MODEL),
        'w_up': nrm(ks[24], (DEPTH, D_MODEL, 2 * D_FF), D_MODEL),
        'ffn_conv_w': nrm(ks[25], (DEPTH, FFN_CONV, 2 * D_FF), FFN_CONV),
        'ffn_conv_b': small(ks[26], (DEPTH, 2 * D_FF)),
        'w_down': nrm(ks[27], (DEPTH, D_FF, D_MODEL), D_FF),
    }


def reference(x, c, w_mod, b_mod, norm_mix, norm_ffn, w_in, qnorm_a, knorm_a, sinks,
              rnn_conv_w, rnn_conv_b, rnn_gate_a_w, rnn_gate_a_b, rnn_gate_x_w, rnn_gate_x_b,
              rnn_lambda, qnorm_c, knorm_c, rel_bias, w_branch, w_out, w_up, ffn_conv_w,
              ffn_conv_b, w_down):
    b_sz, s_len, _ = x.shape
    c_act = jax.nn.silu(c)
    bias_a = rel_bias[:, :SWA_HEADS]
    bias_c = rel_bias[:, SWA_HEADS:]
    for l in range(DEPTH):
        mod = (c_act @ w_mod[l] + b_mod[l])[:, None, :]
        shift_m, scale_m, gate_m, shift_f, scale_f, gate_f = jnp.split(mod, 6, axis=-1)

        h = rms_norm(x, norm_mix[l]) * (1.0 + scale_m) + shift_m
        z = h @ w_in[l]
        qa, ka, va, xr, yr, qc, kc, vc, g_logits = jnp.split(z, IN_OFFSETS, axis=-1)

        qa = rms_norm(qa.reshape(b_sz, s_len, SWA_HEADS, HEAD_DIM), qnorm_a[l])
        ka = rms_norm(ka.reshape(b_sz, s_len, SWA_KV_HEADS, HEAD_DIM), knorm_a[l])
        va = va.reshape(b_sz, s_len, SWA_KV_HEADS, HEAD_DIM)
        o_a = swa_attention(qa, ka, va, sinks[l], bias_a)

        xr = causal_dwconv(xr, rnn_conv_w[l], rnn_conv_b[l])
        hr = rg_lru(xr, rnn_gate_a_w[l], rnn_gate_a_b[l], rnn_gate_x_w[l], rnn_gate_x_b[l], rnn_lambda[l])
        o_b = hr * jax.nn.gelu(yr)

        qc = rms_norm(qc.reshape(b_sz, s_len, MOBA_HEADS, HEAD_DIM), qnorm_c[l])
        kc = rms_norm(kc.reshape(b_sz, s_len, MOBA_HEADS, HEAD_DIM), knorm_c[l])
        vc = vc.reshape(b_sz, s_len, MOBA_HEADS, HEAD_DIM)
        o_c = moba_attention(qc, kc, vc, bias_c)

        wb = w_branch[l]
        p_a = o_a @ wb[:SWA_Q]
        p_b = o_b @ wb[SWA_Q:SWA_Q + RNN_WIDTH]
        p_c = o_c @ wb[SWA_Q + RNN_WIDTH:]
        g = jax.nn.sigmoid(g_logits.reshape(b_sz, s_len, N_BRANCH, D_MODEL))
        merged = g[:, :, 0] * p_a + g[:, :, 1] * p_b + g[:, :, 2] * p_c
        x = x + gate_m * (merged @ w_out[l])

        h = rms_norm(x, norm_ffn[l]) * (1.0 + scale_f) + shift_f
        u = causal_dwconv(h @ w_up[l], ffn_conv_w[l], ffn_conv_b[l])
        u_gate, u_val = jnp.split(u, 2, axis=-1)
        x = x + gate_f * ((jax.nn.silu(u_gate) * u_val) @ w_down[l])
    return x
```

```python
import math
from contextlib import ExitStack
import numpy as np
import concourse.bass as bass
import concourse.mybir as mybir
from concourse.bass_utils import run_bass_kernel_spmd

F32 = mybir.dt.float32
BF16 = mybir.dt.bfloat16
AF = mybir.ActivationFunctionType
ALU = mybir.AluOpType
AX = mybir.AxisListType

D = 1024
DIN = 7424
DFF = 2816
NPP = 244
NEG = -1.0e30
DEBUG = False
DBG = {}
ENGS = ("pe", "act", "dve", "pool", "sp")


class Buf:
    __slots__ = ("w", "r", "name")

    def __init__(self, name=""):
        self.w = None
        self.r = {}
        self.name = name


class WB:
    def __init__(self):
        self.items = []

    def add(self, c0, c1):
        b = Buf()
        self.items.append((c0, c1, b))
        return b

    def cover(self, c0, c1):
        return [b for (a0, a1, b) in self.items if a0 < c1 and c0 < a1]


class Sched:
    def __init__(self, nc, es, nd=16):
        self.nc = nc
        self.ops = {e: [] for e in ENGS}
        self.sem = {}
        self.cnt = {}
        for e in ENGS:
            self.sem[e] = es.enter_context(nc.semaphore("c_" + e))
            self.cnt[e] = 0
        self.nd = {"sp": 12, "pool": 3}
        self.dcur = {"sp": 0, "pool": 0}
        for q in ("sp", "pool"):
            for i in range(self.nd[q]):
                k = "d%s%d" % (q, i)
                self.sem[k] = es.enter_context(nc.semaphore(k))
                self.cnt[k] = 0
        self.seen = {e: {} for e in ENGS}
        self.nins = 0

    def _deps(self, eng, reads, writes):
        need = {}
        seen = self.seen[eng]

        def add(k, v):
            if k == eng and eng == "pe":
                return
            if seen.get(k, 0) < v and need.get(k, 0) < v:
                need[k] = v

        for b in reads:
            if b.w is not None:
                add(*b.w)
        for b in writes:
            if b.w is not None:
                add(*b.w)
            for k, v in b.r.items():
                if k != eng:
                    add(k, v)
        for k, v in need.items():
            seen[k] = v
        return list(need.items())

    def _mark(self, tok, reads, writes):
        k, v = tok
        for b in reads:
            if b.r.get(k, 0) < v:
                b.r[k] = v
        for b in writes:
            b.w = tok
            b.r = {}

    def op(self, eng, fn, reads=(), writes=()):
        waits = self._deps(eng, reads, writes)
        self.cnt[eng] += 1
        self.ops[eng].append((waits, fn, eng, 1))
        self._mark((eng, self.cnt[eng]), reads, writes)

    def dma(self, q, out, in_, reads=(), writes=()):
        k = "d%s%d" % (q, self.dcur[q])
        self.dcur[q] = (self.dcur[q] + 1) % self.nd[q]
        waits = self._deps(q, reads, writes)
        prev = self.cnt[k]
        if prev > 0 and self.seen[q].get(k, 0) < prev:
            waits.append((k, prev))
            self.seen[q][k] = prev
        self.cnt[k] += 16
        self.ops[q].append((waits, lambda e: e.dma_start(out=out, in_=in_), k, 16))
        self._mark((k, self.cnt[k]), reads, writes)

    def barrier(self):
        for e in ENGS:
            waits = []
            for k, v in self.cnt.items():
                if v > 0 and self.seen[e].get(k, 0) < v:
                    waits.append((k, v))
                    self.seen[e][k] = v
            if waits:
                self.ops[e].append((waits, None, None, 0))

    def flush(self):
        nc = self.nc
        ops = self.ops
        sem = self.sem

        def run(name, e):
            for waits, fn, k, inc in ops[name]:
                for wk, wv in waits:
                    e.wait_ge(sem[wk], wv)
                if fn is not None:
                    fn(e).then_inc(sem[k], inc)
                    self.nins += 1

        with nc.Block() as blk:
            @blk.tensor
            def _(e):
                run("pe", e)

            @blk.scalar
            def _(e):
                run("act", e)

            @blk.vector
            def _(e):
                run("dve", e)

            @blk.gpsimd
            def _(e):
                run("pool", e)

            @blk.sync
            def _(e):
                run("sp", e)
        self.ops = {e: [] for e in ENGS}


def TT(o, a, b, op):
    return lambda e: e.tensor_tensor(out=o, in0=a, in1=b, op=op)


def TS(o, a, s1, s2, op0, op1=None):
    if op1 is None:
        return lambda e: e.tensor_scalar(out=o, in0=a, scalar1=s1, scalar2=None, op0=op0)
    return lambda e: e.tensor_scalar(out=o, in0=a, scalar1=s1, scalar2=s2, op0=op0, op1=op1)


def STT(o, a, sc_, b, op0, op1):
    return lambda e: e.scalar_tensor_tensor(out=o, in0=a, scalar=sc_, in1=b, op0=op0, op1=op1)


def AC(o, a, func, **kw):
    return lambda e: e.activation(out=o, in_=a, func=func, **kw)


def CPA(o, a):
    return lambda e: e.copy(out=o, in_=a)


def CPV(o, a):
    return lambda e: e.tensor_copy(out=o, in_=a)


def MS(o, v):
    return lambda e: e.memset(o, v)


def RCP(o, a):
    return lambda e: e.reciprocal(out=o, in_=a)


def MMS(items):
    items = list(items)

    def fn(e):
        ins = None
        for (ps, l, r, st_, sp_) in items:
            ins = e.matmul(ps, lhsT=l, rhs=r, start=st_, stop=sp_, skip_group_check=True)
        return ins
    return fn


def MMC(ps, pairs):
    pairs = list(pairs)
    n = len(pairs)
    return MMS([(ps, l, r, i == 0, i == n - 1) for i, (l, r) in enumerate(pairs)])


def TRS(items, ident):
    items = list(items)

    def fn(e):
        ins = None
        for (o, a) in items:
            ins = e.transpose(o, a, ident)
        return ins
    return fn


def build(S, L):
    nc = bass.Bass("TRN2", target_bir_lowering=False)
    NT = S // 512
    NCH = S // 256
    NKB = S // 128

    def din(name, shape, dt=F32):
        return nc.dram_tensor(name, list(shape), dt, kind="ExternalInput").ap()

    def dscr(name, shape, dt):
        return nc.dram_tensor(name, list(shape), dt, kind=("ExternalOutput" if DEBUG else "Internal")).ap()

    x_in = din("x", [S, D])
    cT_d = din("cT", [128, 8])
    wmod_d = din("w_mod", [L, D, 6 * D])
    bmod_d = din("bmod", [L, 128, 6 * D])
    gains_d = din("gains", [L, 2, 128, D])
    win_d = din("w_in", [L, D, DIN])
    wbr_d = din("w_branch", [L, 2 * D, D])
    wout_d = din("w_out", [L, D, D])
    wup_d = din("w_up", [L, D, 2 * DFF])
    wdn_d = din("w_down", [L, DFF, D])
    pp_d = din("pp", [L, 128, NPP])
    gw_d = din("gw", [L, 128, 2048])
    sink_d = din("sinkrow", [L, 128, 1024])
    oh_d = din("blk1h", [32, S])
    biasA_d = din("biasA", [128, 2048])
    biasC_d = din("biasC", [128, 8 * 2176])
    cfar_d = din("cfar", [128, 8])
    cst_d = din("cst", [128, 384])
    out_d = nc.dram_tensor("out", [S, D], F32, kind="ExternalOutput").ap()

    hT_d = dscr("hT_s", [8, 128, S], BF16)
    oaT_d = dscr("oaT_s", [4, 128, S], BF16)
    obT_d = dscr("obT_s", [8, 128, S], BF16)
    ocT_d = dscr("ocT_s", [4, 128, S], BF16)
    qcT_d = dscr("qcT_s", [4, 128, S], BF16)
    kcT_d = dscr("kcT_s", [4, 128, S], BF16)
    vc_d = dscr("vc_s", [S, 1024], BF16)
    kmT_d = dscr("kmT_s", [128, 4, NCH], BF16)
    mod_d = dscr("mod_s", [L, 128, 6 * D], F32)
    xa_d = dscr("xa_s", [S, D], F32)
    xb_d = dscr("xb_s", [S, D], F32)

    es = ExitStack()
    with es:
        sc = Sched(nc, es)

        uid = [0]

        def sb(st, name, shape, dt):
            uid[0] += 1
            return st.enter_context(nc.sbuf_tensor("s%d_%s" % (uid[0], name), list(shape), dt))

        def psb(st, name, shape, dt=F32):
            uid[0] += 1
            return st.enter_context(nc.psum_tensor("p%d_%s" % (uid[0], name), list(shape), dt))

        def ring(st, name, shape, dt, n=2, ps=False):
            mk = psb if ps else sb
            return [mk(st, "%s%d" % (name, i), shape, dt) for i in range(n)], [Buf() for _ in range(n)]

        cstf = sb(es, "cstf", [128, 384], F32)
        ident = sb(es, "ident", [128, 128], BF16)
        bones = sb(es, "bones", [128, 128], BF16)
        ones_f = cstf[:, 256:384]
        ones_row = cstf[64:65, 256:320]
        cact = sb(es, "cact", [128, 8], F32)
        eps_t = sb(es, "eps_t", [128, 1], F32)
        B_c = Buf("const")
        sc.op("pool", MS(eps_t[:], 1e-6), writes=[B_c])
        sc.dma("sp", cstf[:], cst_d[:, :], writes=[B_c])
        sc.dma("sp", cact[:], cT_d[:, :], writes=[B_c])
        sc.op("dve", CPV(ident[:], cstf[:, 0:128]), reads=[B_c], writes=[B_c])
        sc.op("dve", CPV(bones[:], cstf[:, 128:256]), reads=[B_c], writes=[B_c])
        sc.op("act", AC(cact[:], cact[:], AF.Silu), reads=[B_c], writes=[B_c])

        with ExitStack() as st:
            cbc = sb(st, "cbc", [128, 8, 128], F32)
            wm, B_wm = ring(st, "wm", [128, 8, 512], F32)
            bm, B_bm = ring(st, "bm", [128, 512], F32)
            gn = sb(st, "gn", [128, 2, D], F32)
            modt = sb(st, "modt", [128, 6 * D], F32)
            zp, B_zp = ring(st, "p0z", [128, 512], F32, ps=True)
            B_cbc, B_gn, B_modt = Buf(), Buf(), Buf()
            for kc in range(8):
                sc.op("dve", TS(cbc[:, kc, :], ones_f, cact[:, kc:kc + 1], None, ALU.mult), reads=[B_c], writes=[B_cbc])
            for l in range(L):
                sc.dma("sp", gn[:], gains_d[l].rearrange("t p n -> p t n"), writes=[B_gn])
                for g in range(12):
                    i = g % 2
                    sc.dma("sp", wm[i][:], wmod_d[l, :, g * 512:(g + 1) * 512].rearrange("(kc p) n -> p kc n", p=128),
                           writes=[B_wm[i]])
                    sc.dma("pool", bm[i][:], bmod_d[l, :, g * 512:(g + 1) * 512], writes=[B_bm[i]])
                    sc.op("pe", MMC(zp[i][:], [(cbc[:, kc, :], wm[i][:, kc, :]) for kc in range(8)]),
                          reads=[B_cbc, B_wm[i]], writes=[B_zp[i]])
                    sc.op("dve", TT(modt[:, g * 512:(g + 1) * 512], zp[i][:], bm[i][:], ALU.add),
                          reads=[B_zp[i], B_bm[i]], writes=[B_modt])
                for (o, t) in ((1024, 0), (4096, 1)):
                    sc.op("dve", STT(modt[:, o:o + 1024], modt[:, o:o + 1024], 1.0, gn[:, t, :], ALU.add, ALU.mult),
                          reads=[B_modt, B_gn], writes=[B_modt])
                sc.dma("sp", mod_d[l], modt[:], reads=[B_modt])
            sc.barrier()
            sc.flush()

        stgG, B_stgG = ring(es, "stgG", [128, 2048], F32)
        nld = [0]

        def load_weights(dst, wb, src, c0, w):
            nk = dst.shape[1]
            cw = 2048 // nk
            for o in range(0, w, cw):
                ww = min(cw, w - o)
                i = nld[0] % 2
                nld[0] += 1
                sv = stgG[i][:, 0:nk * cw].rearrange("p (k n) -> p k n", k=nk)[:, :, 0:ww]
                sc.dma("sp", sv, src[:, o:o + ww].rearrange("(kc p) n -> p kc n", p=128), writes=[B_stgG[i]])
                bw = wb.add(c0 + o, c0 + o + ww)
                if i == 0:
                    sc.op("act", CPA(dst[:, :, c0 + o:c0 + o + ww], sv), reads=[B_stgG[i]], writes=[bw])
                else:
                    sc.op("pool", CPV(dst[:, :, c0 + o:c0 + o + ww], sv), reads=[B_stgG[i]], writes=[bw])

        def norm_sub(t, sub, xsrc, G, Sh, B_mod, xt, B_xt, junk, ss, B_ss, tmp, B_tmp, hb, B_hb, tp, B_tp, hT, B_hT, q=None):
            i = sub % 2
            r0 = t * 512 + sub * 128
            if q in (None, 0):
                sc.dma("sp", xt[i][:], xsrc[r0:r0 + 128, :], writes=[B_xt[i]])
                sc.op("pool", MS(ss[:], 0.0), writes=[B_ss])
                sc.op("act", AC(junk[:], xt[i][:], AF.Square, accum_out=ss[:]), reads=[B_xt[i]], writes=[B_ss])
            if q in (None, 1):
                sc.op("act", AC(ss[:], ss[:], AF.Ln, scale=1.0 / D, bias=eps_t[:, 0:1]), reads=[B_ss, B_c], writes=[B_ss])
                sc.op("act", AC(ss[:], ss[:], AF.Exp, scale=-0.5), reads=[B_ss], writes=[B_ss])
                sc.op("dve", STT(tmp[:], xt[i][:], ss[:, 0:1], G, ALU.mult, ALU.mult), reads=[B_xt[i], B_ss, B_mod],
                      writes=[B_tmp])
            if q in (None, 2):
                sc.op("pool", TT(hb[:], tmp[:], Sh, ALU.add), reads=[B_tmp, B_mod], writes=[B_hb])
            if q in (None, 3):
                sc.op("pe", TRS([(tp[:, kc, :], hb[:, kc * 128:(kc + 1) * 128]) for kc in range(8)], ident[:]),
                      reads=[B_hb, B_c], writes=[B_tp])
                sc.op("act", CPA(hT[:, :, sub * 128:(sub + 1) * 128], tp[:]), reads=[B_tp], writes=[B_hT])

        for l in range(L):
            x_l = x_in if l == 0 else out_d

            with ExitStack() as st:
                wA = sb(st, "wA", [128, 8, 2304], BF16)
                B_wA = WB()
                load_weights(wA, B_wA, win_d[l][:, 0:768], 0, 768)
                load_weights(wA, B_wA, win_d[l][:, 2816:4352], 768, 1536)
                modm = sb(st, "modm", [128, 2048], F32)
                B_mod = Buf()
                sc.dma("sp", modm[:], mod_d[l][:, 0:2048], writes=[B_mod])
                pp = sb(st, "pp", [128, NPP], F32)
                B_pp = Buf()
                sc.dma("sp", pp[:], pp_d[l], writes=[B_pp])
                gq = sb(st, "gq", [128, 2], F32)
                sc.op("dve", TS(gq[:, 0:1], pp[:, 0:1], 0.125, None, ALU.mult), reads=[B_pp], writes=[B_pp])
                sc.op("dve", TS(gq[:, 1:2], pp[:, 2:3], 0.125, None, ALU.mult), reads=[B_pp], writes=[B_pp])
                bA = sb(st, "bA", [128, 8, 2, 128], F32)
                sc.dma("sp", bA[:], biasA_d.rearrange("p (h k q) -> p h k q", h=8, k=2), writes=[B_pp])
                esk = sb(st, "esk", [128, 1024], F32)
                sc.dma("sp", esk[:], sink_d[l], writes=[B_pp])
                sc.op("act", AC(esk[:], esk[:], AF.Exp), reads=[B_pp], writes=[B_pp])

                xt, B_xt = ring(st, "xt", [128, D], F32)
                junk = sb(st, "junk", [128, D], F32)
                ss = sb(st, "ss", [128, 1], F32)
                tmp = sb(st, "tmp", [128, D], F32)
                hb = sb(st, "hb", [128, D], BF16)
                hT, B_hT = ring(st, "hT", [128, 8, 512], BF16)
                B_ss, B_tmp, B_hb = Buf(), Buf(), Buf()
                qaT = sb(st, "qaT", [128, 4, 512], BF16)
                B_qaT = Buf()
                kn = sb(st, "kn", [128, 512], BF16)
                B_kn = Buf()
                kaP = [[sb(st, "kaP%d%d" % (g, h), [128, 640], BF16) for h in range(2)] for g in range(2)]
                B_kaP = Buf()
                vaA = sb(st, "vaA", [128, 5, 2, 128], BF16)
                B_vaA = Buf()
                for g in range(2):
                    for h in range(2):
                        sc.op("pool", MS(kaP[g][h][:], 0.0), writes=[B_kaP])
                sc.op("pool", MS(vaA[:], 1.0), writes=[B_vaA])
                qcb, B_qcb = ring(st, "qcb", [128, 4, 512], BF16)
                kcb, B_kcb = ring(st, "kcb", [128, 4, 512], BF16)
                vcb, B_vcb = ring(st, "vcb", [128, 4, 8, 128], BF16)
                for i in range(2):
                    sc.op("pool", MS(vcb[i][:], 1.0), writes=[B_vcb[i]])
                kms = sb(st, "kms", [128, 4, 2], F32)
                B_kms = Buf()
                kmb, B_kmb = ring(st, "kmb", [128, 4, 2], BF16)
                sqb, B_sqb = ring(st, "sqb", [128, 512], BF16)
                rs, B_rs = ring(st, "rs", [128, 512], F32)
                Sb, B_Sb = ring(st, "Sb", [128, 4, 2, 128], F32)
                PT, B_PT = ring(st, "PT", [128, 4, 2, 128], BF16)
                dn, B_dn = ring(st, "dn", [128, 2, 2, 128], F32)
                oaT, B_oaT = ring(st, "oaT", [128, 4, 512], BF16)
                tp = psb(st, "tp", [128, 8, 128], BF16)
                Zt = psb(st, "Zt", [128, 2, 512])
                zp = [Zt[:, 0, :], Zt[:, 1, :]]
                B_zp = [Buf(), Buf()]
                msp = psb(st, "msp", [128, 512])
                Sp0 = psb(st, "Sp", [128, 4, 2, 128])
                Sp = [Sp0, Zt[:, :, :].rearrange("p a n -> p (a n)").rearrange("p (h k q) -> p h k q", h=4, k=2)]
                B_Sp = [[Buf()], B_zp]
                Op, B_Op = ring(st, "Op", [128, 2, 2, 128], F32, ps=True)
                B_tp, B_msp = Buf(), Buf()
                Sh_m = modm[:, 0:1024]
                G_m = modm[:, 1024:2048]
                nz = 0
                for t in range(NT):
                    hi = t % 2
                    ts = slice(t * 512, (t + 1) * 512)
                    def nrm(tt, sub, q=None):
                        if tt >= NT:
                            return
                        h2 = tt % 2
                        norm_sub(tt, sub, x_l, G_m, Sh_m, B_mod, xt, B_xt, junk, ss, B_ss, tmp, B_tmp, hb, B_hb, tp, B_tp,
                                 hT[h2], B_hT[h2], q=q)
                        if sub == 3 and q in (None, 3):
                            sc.dma("pool", hT_d[:, :, tt * 512:(tt + 1) * 512].rearrange("kc p s -> p kc s"), hT[h2][:],
                                   reads=[B_hT[h2]])
                    nsteps = [(s_, q_) for s_ in range(4) for q_ in range(4)]
                    npts = {}
                    for s_, q_ in nsteps:
                        npts.setdefault(min(3 * s_ + q_ + 1, 15), []).append((s_, q_))
                    if t == 0:
                        for sub in range(4):
                            nrm(0, sub)
                    plan = [("qa", c, c * 128) for c in range(4)] + [("ka", 0, 512)] + \
                           [("qc", c, 768 + c * 128) for c in range(4)] + [("kc", c, 1280 + c * 128) for c in range(4)]
                    for pi_, (kind, c, wc) in enumerate(plan):
                        for (s_, q_) in npts.get(pi_, []):
                            nrm(t + 1, s_, q_)
                        zi = nz % 2
                        nz += 1
                        sc.op("pe", MMC(zp[zi][:], [(wA[:, kc, wc:wc + 128], hT[hi][:, kc, :]) for kc in range(8)]),
                              reads=B_wA.cover(wc, wc + 128) + [B_hT[hi]], writes=[B_zp[zi]])
                        sc.op("act", AC(sqb[zi][:], zp[zi][:], AF.Square), reads=[B_zp[zi]], writes=[B_sqb[zi]])
                        sc.op("pe", MMC(msp[:], [(bones[:], sqb[zi][:])]), reads=[B_c, B_sqb[zi]], writes=[B_msp])
                        sc.op("act", AC(rs[zi][:], msp[:], AF.Ln, bias=eps_t[:, 0:1]), reads=[B_msp, B_c], writes=[B_rs[zi]])
                        sc.op("act", AC(rs[zi][:], rs[zi][:], AF.Exp, scale=-0.5), reads=[B_rs[zi]], writes=[B_rs[zi]])
                        if kind == "qa":
                            dst, B_dst, gcol = qaT[:, c, :], B_qaT, gq[:, 0:1]
                        elif kind == "ka":
                            dst, B_dst, gcol = kn[:], B_kn, pp[:, 1:2]
                        elif kind == "qc":
                            dst, B_dst, gcol = qcb[hi][:, c, :], B_qcb[hi], gq[:, 1:2]
                        else:
                            dst, B_dst, gcol = kcb[hi][:, c, :], B_kcb[hi], pp[:, 3:4]
                        sc.op("dve", STT(dst, zp[zi][:], gcol, rs[zi][:], ALU.mult, ALU.mult),
                              reads=[B_zp[zi], B_rs[zi], B_pp], writes=[B_dst])
                        if kind == "ka":
                            for g in range(2):
                                for h in range(2):
                                    o_ap = kaP[g][h][h * 64:(h + 1) * 64, 128:640]
                                    i_ap = kn[g * 64:(g + 1) * 64, :]
                                    if h == 0:
                                        sc.op("act", CPA(o_ap, i_ap), reads=[B_kn], writes=[B_kaP])
                                    else:
                                        sc.op("dve", CPV(o_ap, i_ap), reads=[B_kn], writes=[B_kaP])
                        if kind == "kc":
                            sc.op("dve", lambda e, o_=kms[:, c, :], i_=kcb[hi][:, c, :].rearrange("p (a b) -> p a b", a=2):
                                  e.reduce_sum(out=o_, in_=i_, axis=AX.X), reads=[B_kcb[hi]], writes=[B_kms])
                    sc.op("dve", TS(kmb[hi][:], kms[:], 1.0 / 256, None, ALU.mult), reads=[B_kms], writes=[B_kmb[hi]])
                    sc.dma("pool", kmT_d[:, :, 2 * t:2 * t + 2], kmb[hi][:], reads=[B_kmb[hi]])
                    sc.dma("pool", qcT_d[:, :, ts].rearrange("c p s -> p c s"), qcb[hi][:], reads=[B_qcb[hi]])
                    sc.dma("pool", kcT_d[:, :, ts].rearrange("c p s -> p c s"), kcb[hi][:], reads=[B_kcb[hi]])
                    for sub in range(4):
                        tsl = slice(sub * 128, (sub + 1) * 128)
                        zi = nz % 2
                        nz += 1
                        sc.op("pe", MMC(zp[zi][:, 0:128], [(hT[hi][:, kc, tsl], wA[:, kc, 640:768]) for kc in range(8)]),
                              reads=B_wA.cover(640, 768) + [B_hT[hi]], writes=[B_zp[zi]])
                        sc.op("act", CPA(vaA[:, 1 + sub, :, 0:64], zp[zi][:, 0:128].rearrange("p (g d) -> p g d", g=2)),
                              reads=[B_zp[zi]], writes=[B_vaA])
                        zi = nz % 2
                        nz += 1
                        sc.op("pe", MMC(zp[zi][:], [(hT[hi][:, kc, tsl], wA[:, kc, 1792:2304]) for kc in range(8)]),
                              reads=B_wA.cover(1792, 2304) + [B_hT[hi]], writes=[B_zp[zi]])
                        sc.op("act", CPA(vcb[hi][:, sub, :, 0:64], zp[zi][:].rearrange("p (h d) -> p h d", h=8)),
                              reads=[B_zp[zi]], writes=[B_vcb[hi]])
                    sc.dma("pool", vc_d[ts, :].rearrange("(s p) f -> p s f", p=128),
                           vcb[hi][:].rearrange("p s h d -> p s (h d)"), reads=[B_vcb[hi]])
                    for pt_ in (13, 14, 15):
                        for (s_, q_) in npts.get(pt_, []):
                            nrm(t + 1, s_, q_)
                    its = [(i, g) for i in range(4) for g in range(2)]

                    def swa_x(n_):
                        i, g = its[n_]
                        si = n_ % 2
                        gb = 4 * t + i
                        kbs = [1] if gb == 0 else [0, 1]
                        k0 = kbs[0]
                        qs = slice(i * 128, (i + 1) * 128)
                        items = []
                        for hh in range(4):
                            h = 4 * g + hh
                            for kb in kbs:
                                kc0 = (i + kb) * 128
                                items.append((Sp[si][:, hh, kb, :], kaP[g][h % 2][:, kc0:kc0 + 128], qaT[:, h // 2, qs], True, True))
                        sc.op("pe", MMS(items), reads=[B_kaP, B_qaT], writes=B_Sp[si])
                        sc.op("dve", TT(Sb[si][:, :, k0:2, :], Sp[si][:, :, k0:2, :], bA[:, 4 * g:4 * g + 4, k0:2, :], ALU.add),
                              reads=B_Sp[si] + [B_pp], writes=[B_Sb[si]])
                        sc.op("act", AC(PT[si][:, :, k0:2, :], Sb[si][:, :, k0:2, :], AF.Exp), reads=[B_Sb[si]], writes=[B_PT[si]])

                    def swa_y(n_):
                        i, g = its[n_]
                        si = n_ % 2
                        gb = 4 * t + i
                        kbs = [1] if gb == 0 else [0, 1]
                        qs = slice(i * 128, (i + 1) * 128)
                        items = []
                        for hh in range(4):
                            for m_, kb in enumerate(kbs):
                                items.append((Op[si][:, hh // 2, hh % 2, :], vaA[:, i + kb, g, :], PT[si][:, hh, kb, :],
                                              m_ == 0, m_ == len(kbs) - 1))
                        sc.op("pe", MMS(items), reads=[B_vaA, B_PT[si]], writes=[B_Op[si]])
                        dnf = dn[si][64:128, :, :, :].rearrange("p a b q -> p (a b q)")
                        sc.op("dve", TT(dnf, Op[si][64:128, :, :, :].rearrange("p a b q -> p (a b q)"),
                                        esk[64:128, g * 512:(g + 1) * 512], ALU.add), reads=[B_Op[si], B_pp], writes=[B_dn[si]])
                        sc.op("act", AC(dnf, dnf, AF.Ln), reads=[B_dn[si]], writes=[B_dn[si]])
                        sc.op("act", AC(dnf, dnf, AF.Exp, scale=-1.0), reads=[B_dn[si]], writes=[B_dn[si]])
                        for half in range(2):
                            sc.op("dve", TT(oaT[hi][half * 64:(half + 1) * 64, 2 * g:2 * g + 2, qs],
                                            Op[si][0:64, :, half, :], dn[si][64:128, :, half, :], ALU.mult),
                                  reads=[B_Op[si], B_dn[si]], writes=[B_oaT[hi]])

                    swa_x(0)
                    for n_ in range(len(its)):
                        if n_ + 1 < len(its):
                            swa_x(n_ + 1)
                        swa_y(n_)
                    sc.dma("pool", oaT_d[:, :, ts].rearrange("c p s -> p c s"), oaT[hi][:], reads=[B_oaT[hi]])
                    if t + 1 < NT:
                        for g in range(2):
                            for h in range(2):
                                sc.op("pool", CPV(kaP[g][h][:, 0:128], kaP[g][h][:, 512:640]), reads=[B_kaP], writes=[B_kaP])
                        sc.op("pool", CPV(vaA[:, 0, :, :], vaA[:, 4, :, :]), reads=[B_vaA], writes=[B_vaA])
                sc.barrier()
                sc.flush()

            with ExitStack() as st:
                wB = sb(st, "wB", [128, 8, 2048], BF16)
                B_wB = WB()
                load_weights(wB, B_wB, win_d[l][:, 768:1280], 0, 512)
                load_weights(wB, B_wB, win_d[l][:, 1792:2304], 1024, 512)
                load_weights(wB, B_wB, win_d[l][:, 1280:1792], 512, 512)
                load_weights(wB, B_wB, win_d[l][:, 2304:2816], 1536, 512)
                pp = sb(st, "pp2", [128, NPP], F32)
                B_pp = Buf()
                sc.dma("sp", pp[:], pp_d[l], writes=[B_pp])
                gwf = sb(st, "gwf", [128, 2048], F32)
                gwb = sb(st, "gwb", [128, 2, 8, 128], BF16)
                sc.dma("sp", gwf[:], gw_d[l], writes=[B_pp])
                sc.op("dve", CPV(gwb[:].rearrange("p a c n -> p (a c n)"), gwf[:]), reads=[B_pp], writes=[B_pp])
                cA = sb(st, "cA", [128, 16], F32)
                sc.op("act", AC(cA[:, 0:8], pp[:, 60:68], AF.Exp, scale=-1.0), reads=[B_pp], writes=[B_pp])
                sc.op("act", AC(cA[:, 0:8], cA[:, 0:8], AF.Ln, bias=1.0), reads=[B_pp], writes=[B_pp])
                sc.op("dve", TS(cA[:, 8:16], cA[:, 0:8], -16.0, None, ALU.mult), reads=[B_pp], writes=[B_pp])
                sc.op("dve", TS(cA[:, 0:8], cA[:, 0:8], -8.0, None, ALU.mult), reads=[B_pp], writes=[B_pp])
                halo = sb(st, "halo", [128, 8, 3], F32)
                state = sb(st, "state", [128, 8], F32)
                B_halo, B_state = Buf(), Buf()
                sc.op("pool", MS(halo[:], 0.0), writes=[B_halo])
                sc.op("pool", MS(state[:], 0.0), writes=[B_state])
                hT, B_hT = ring(st, "hTb", [128, 8, 512], BF16)
                GC = 2
                xs, B_xs = ring(st, "xs", [128, GC, 515], F32)
                yb, B_yb = ring(st, "yb", [128, GC, 512], F32, n=3)
                xc, B_xc = ring(st, "xc", [128, GC, 512], F32, n=3)
                xcb, B_xcb = ring(st, "xcb", [128, GC, 512], BF16)
                rr, B_rr = ring(st, "rr", [128, GC, 512], F32)
                ii, B_ii = ring(st, "ii", [128, GC, 512], F32)
                aa, B_aa = ring(st, "aa", [128, GC, 512], F32)
                a2, B_a2 = ring(st, "a2", [128, GC, 512], F32)
                hh_, B_hh = ring(st, "hh", [128, GC, 512], F32)
                ge, B_ge = ring(st, "ge", [128, GC, 512], F32, n=3)
                ob, B_ob = ring(st, "ob", [128, 8, 512], BF16)
                zx, B_zx = ring(st, "zx", [128, GC, 512], F32, ps=True)
                zy, B_zy = ring(st, "zy", [128, GC, 512], F32, ps=True)
                NG = 8 // GC

                def stP(t, g, n):
                    hi, k, k9 = t % 2, n % 2, n % 3
                    C0 = GC * g
                    if g == 0 and t == 0:
                        sc.dma("sp", hT[0][:], hT_d[:, :, 0:512].rearrange("kc p s -> p kc s"), writes=[B_hT[0]])
                    for (zz, B_zz, wo) in ((zx, B_zx, 0), (zy, B_zy, 1024)):
                        items = []
                        for c in range(GC):
                            C = C0 + c
                            for kc in range(8):
                                items.append((zz[k][:, c, :], wB[:, kc, wo + C * 128:wo + (C + 1) * 128], hT[hi][:, kc, :],
                                              kc == 0, kc == 7))
                        sc.op("pe", MMS(items), reads=B_wB.cover(wo + C0 * 128, wo + (C0 + GC) * 128) + [B_hT[hi]], writes=[B_zz[k]])
                    if g == 0 and t + 1 < NT:
                        h2 = (t + 1) % 2
                        sc.dma("sp", hT[h2][:], hT_d[:, :, (t + 1) * 512:(t + 2) * 512].rearrange("kc p s -> p kc s"),
                               writes=[B_hT[h2]])
                    sc.op("dve", CPV(xs[k][:, :, 3:515], zx[k][:]), reads=[B_zx[k]], writes=[B_xs[k]])
                    sc.op("dve", CPV(yb[k9][:], zy[k][:]), reads=[B_zy[k]], writes=[B_yb[k9]])
                    sc.op("pool", CPV(xs[k][:, :, 0:3], halo[:, C0:C0 + GC, :]), reads=[B_halo], writes=[B_xs[k]])
                    sc.op("pool", CPV(halo[:, C0:C0 + GC, :], xs[k][:, :, 512:515]), reads=[B_xs[k]], writes=[B_halo])
                    sc.op("act", AC(ge[k9][:], yb[k9][:], AF.Square), reads=[B_yb[k9]], writes=[B_ge[k9]])

                def stB(t, g, n):
                    hi, k, k9 = t % 2, n % 2, n % 3
                    C0 = GC * g
                    sc.op("pool", TS(ge[k9][:], ge[k9][:], 0.044715, 1.0, ALU.mult, ALU.add), reads=[B_ge[k9]], writes=[B_ge[k9]])
                    sc.op("dve", TT(ge[k9][:], ge[k9][:], yb[k9][:], ALU.mult), reads=[B_ge[k9], B_yb[k9]], writes=[B_ge[k9]])
                    for c in range(GC):
                        C = C0 + c
                        sc.op("pool", TS(xc[k9][:, c, :], xs[k][:, c, 0:512], pp[:, 4 + 4 * C:5 + 4 * C], pp[:, 36 + C:37 + C],
                                         ALU.mult, ALU.add), reads=[B_xs[k], B_pp], writes=[B_xc[k9]])
                    for c in range(GC):
                        C = C0 + c
                        for kk in (1, 2, 3):
                            sc.op("dve", STT(xc[k9][:, c, :], xs[k][:, c, kk:kk + 512], pp[:, 4 + 4 * C + kk:5 + 4 * C + kk],
                                             xc[k9][:, c, :], ALU.mult, ALU.add), reads=[B_xs[k], B_pp, B_xc[k9]], writes=[B_xc[k9]])
                    sc.op("act", CPA(xcb[k][:], xc[k9][:]), reads=[B_xc[k9]], writes=[B_xcb[k]])
                    sc.op("pe", MMS([(zx[k][:, c, :], gwb[:, 0, C0 + c, :], xcb[k][:, c, :], True, True) for c in range(GC)]),
                          reads=[B_pp, B_xcb[k]], writes=[B_zx[k]])
                    sc.op("pe", MMS([(zy[k][:, c, :], gwb[:, 1, C0 + c, :], xcb[k][:, c, :], True, True) for c in range(GC)]),
                          reads=[B_pp, B_xcb[k]], writes=[B_zy[k]])

                def stQ(t, g, n):
                    k, k9 = n % 2, n % 3
                    C0 = GC * g
                    sc.op("act", AC(ge[k9][:], ge[k9][:], AF.Sigmoid, scale=1.5957691216057308), reads=[B_ge[k9]], writes=[B_ge[k9]])
                    for c in range(GC):
                        C = C0 + c
                        sc.op("act", AC(rr[k][:, c, :], zx[k][:, c, :], AF.Sigmoid, bias=pp[:, 44 + C:45 + C]),
                              reads=[B_zx[k], B_pp], writes=[B_rr[k]])
                    for c in range(GC):
                        C = C0 + c
                        sc.op("act", AC(ii[k][:, c, :], zy[k][:, c, :], AF.Sigmoid, bias=pp[:, 52 + C:53 + C]),
                              reads=[B_zy[k], B_pp], writes=[B_ii[k]])
                    for c in range(GC):
                        C = C0 + c
                        sc.op("act", AC(aa[k][:, c, :], rr[k][:, c, :], AF.Exp, scale=cA[:, C:C + 1]), reads=[B_rr[k], B_pp],
                              writes=[B_aa[k]])
                    for c in range(GC):
                        C = C0 + c
                        sc.op("act", AC(a2[k][:, c, :], rr[k][:, c, :], AF.Exp, scale=cA[:, 8 + C:9 + C]), reads=[B_rr[k], B_pp],
                              writes=[B_a2[k]])
                    sc.op("act", AC(a2[k][:], a2[k][:], AF.Ln, scale=-1.0, bias=1.0), reads=[B_a2[k]], writes=[B_a2[k]])
                    sc.op("act", AC(a2[k][:], a2[k][:], AF.Exp, scale=0.5), reads=[B_a2[k]], writes=[B_a2[k]])

                def stR(t, g, n):
                    hi, k, k9 = t % 2, n % 2, n % 3
                    C0 = GC * g
                    if t == 0:
                        sc.op("dve", MS(a2[k][:, :, 0:1], 1.0), reads=[B_a2[k]], writes=[B_a2[k]])
                    sc.op("pool", TT(ii[k][:], ii[k][:], xc[k9][:], ALU.mult), reads=[B_ii[k], B_xc[k9]], writes=[B_ii[k]])
                    sc.op("pool", TT(ge[k9][:], ge[k9][:], yb[k9][:], ALU.mult), reads=[B_ge[k9], B_yb[k9]], writes=[B_ge[k9]])
                    sc.op("dve", TT(ii[k][:], ii[k][:], a2[k][:], ALU.mult), reads=[B_ii[k], B_a2[k]], writes=[B_ii[k]])
                    for c in range(GC):
                        C = C0 + c
                        sc.op("dve", lambda e, o_=hh_[k][:, c, :], a_=aa[k][:, c, :], u_=ii[k][:, c, :], s_=state[:, C:C + 1]:
                              e.tensor_tensor_scan(out=o_, data0=a_, data1=u_, initial=s_, op0=ALU.mult, op1=ALU.add),
                              reads=[B_aa[k], B_ii[k], B_state], writes=[B_hh[k]])
                    sc.op("dve", CPV(state[:, C0:C0 + GC].rearrange("p (c o) -> p c o", o=1), hh_[k][:, :, 511:512]),
                          reads=[B_hh[k]], writes=[B_state])
                    sc.op("dve", TT(ob[hi][:, C0:C0 + GC, :], hh_[k][:], ge[k9][:], ALU.mult), reads=[B_hh[k], B_ge[k9]],
                          writes=[B_ob[hi]])
                    if g == NG - 1:
                        sc.dma("pool", obT_d[:, :, t * 512:(t + 1) * 512].rearrange("c p s -> p c s"), ob[hi][:],
                               reads=[B_ob[hi]])

                U = [(t, g) for t in range(NT) for g in range(NG)]
                for s_ in range(len(U) + 2):
                    for fn, off in ((stP, 0), (stQ, 1), (stR, 2), (stB, 0)):
                        idx = s_ - off
                        if 0 <= idx < len(U):
                            fn(U[idx][0], U[idx][1], idx)
                sc.barrier()
                sc.flush()

            for hg in range(2):
                with ExitStack() as st:
                    KTa = sb(st, "KTa", [128, 4, S], BF16)
                    VA = sb(st, "VA", [128, NKB, 4, 128], BF16)
                    B_k = Buf()
                    PW = min(2048, S)
                    NPC = S // PW
                    B_KT = [Buf() for _ in range(NPC)]
                    B_VA = [Buf() for _ in range(NPC)]
                    ohs, B_ohs = [s_[0:32, :] for s_ in stgG], B_stgG
                    for pi in range(NPC):
                        o = pi * PW
                        for hl in range(4):
                            h = 4 * hg + hl
                            sc.dma("sp", KTa[0:64, hl, o:o + PW], kcT_d[h // 2, (h % 2) * 64:(h % 2) * 64 + 64, o:o + PW],
                                   writes=[B_KT[pi]])
                        i = pi % 2
                        sc.dma("sp", ohs[i][:, 0:PW], oh_d[:, o:o + PW], writes=[B_ohs[i]])
                        for hl in range(4):
                            if hl % 2 == 0:
                                sc.op("act", CPA(KTa[64:96, hl, o:o + PW], ohs[i][:, 0:PW]), reads=[B_ohs[i]], writes=[B_KT[pi]])
                            else:
                                sc.op("dve", CPV(KTa[64:96, hl, o:o + PW], ohs[i][:, 0:PW]), reads=[B_ohs[i]], writes=[B_KT[pi]])
                        nv = PW // 128
                        sc.dma("sp", VA[:, pi * nv:(pi + 1) * nv, :, :],
                               vc_d[o:o + PW, hg * 512:(hg + 1) * 512].rearrange("(s p) (h d) -> p s h d", p=128, h=4),
                               writes=[B_VA[pi]])
                    bC = sb(st, "bC", [128, 4, 2176], BF16)
                    bst, B_bst = [s_[:, 0:544] for s_ in stgG], B_stgG
                    nb_ = 0
                    for hl in range(4):
                        h = 4 * hg + hl
                        for o in range(0, 2176, 544):
                            i = nb_ % 2
                            nb_ += 1
                            sc.dma("sp", bst[i][:], biasC_d[:, h * 2176 + o:h * 2176 + o + 544], writes=[B_bst[i]])
                            if i == 0:
                                sc.op("act", CPA(bC[:, hl, o:o + 544], bst[i][:]), reads=[B_bst[i]], writes=[B_k])
                            else:
                                sc.op("dve", CPV(bC[:, hl, o:o + 544], bst[i][:]), reads=[B_bst[i]], writes=[B_k])
                    kmh = sb(st, "kmh", [64, 4, NCH], BF16)
                    for hl in range(4):
                        h = 4 * hg + hl
                        sc.dma("sp", kmh[:, hl, :], kmT_d[(h % 2) * 64:(h % 2) * 64 + 64, h // 2, :], writes=[B_k])
                    cfar = sb(st, "cfar", [128, 8], F32)
                    sc.dma("sp", cfar[:], cfar_d[:, :], writes=[B_k])
                    Qa, B_Qa = ring(st, "Qa", [128, 4, 512], BF16)
                    for i in range(2):
                        sc.op("pool", MS(Qa[i][:], 0.0), writes=[B_Qa[i]])
                    gs = sb(st, "gs", [128, 2, 4, 32], F32)
                    m8 = sb(st, "m8", [128, 2, 4, 8], F32)
                    mbf = sb(st, "mbf", [128, 2, 4, 32], F32)
                    mb = sb(st, "mb", [128, 2, 4, 32], BF16)
                    B_gs, B_m8, B_mbf, B_mb = Buf(), Buf(), Buf(), Buf()
                    sc.op("pool", MS(gs[:], NEG), writes=[B_gs])
                    PT, B_PT = ring(st, "PTb", [128, 2, 512], BF16, n=3)
                    rcs, B_rcs = ring(st, "rcs", [128, 512], F32)
                    ocb, B_ocb = ring(st, "ocb", [128, 2, 512], BF16)
                    misc = psb(st, "misc", [128, 512])
                    gp = misc[:, 0:256].rearrange("p (a b c) -> p a b c", a=2, b=4)
                    tpm = psb(st, "tpm", [32, 4, 256], BF16)
                    Sp, B_Sp = ring(st, "Spb", [128, 2, 512], F32, ps=True)
                    Op, B_Op = ring(st, "Opb", [128, 512], F32, ps=True)
                    B_gp, B_tpm = Buf(), Buf()
                    nU = 0
                    nH = 0
                    for m in range(NCH // 2):
                        n0, n1 = 2 * m, 2 * m + 1
                        qi = m % 2
                        oi = m % 2

                        def q_load(mm):
                            q2 = mm % 2
                            for hl in range(4):
                                h = 4 * hg + hl
                                sc.dma("sp", Qa[q2][0:64, hl, :],
                                       qcT_d[h // 2, (h % 2) * 64:(h % 2) * 64 + 64, mm * 512:(mm + 1) * 512], writes=[B_Qa[q2]])

                        def mask_a(mm, e):
                            q2 = mm % 2
                            n = 2 * mm + e
                            if n >= 1:
                                items = []
                                for qh in range(2):
                                    for hl in range(4):
                                        q0 = e * 256 + qh * 128
                                        items.append((gp[:, qh, hl, 0:n], Qa[q2][0:64, hl, q0:q0 + 128], kmh[:, hl, 0:n], True, True))
                                sc.op("pe", MMS(items), reads=[B_Qa[q2], B_k], writes=[B_gp])
                                sc.op("dve", CPV(gs[:, :, :, 0:n], gp[:, :, :, 0:n]), reads=[B_gp], writes=[B_gs])
                                for qh in range(2):
                                    for hl in range(4):
                                        sc.op("dve", lambda e_, o_=m8[:, qh, hl, :], i_=gs[:, qh, hl, :]: e_.max(out=o_, in_=i_),
                                              reads=[B_gs], writes=[B_m8])
                                for qh in range(2):
                                    for hl in range(4):
                                        sc.op("dve", TS(mbf[:, qh, hl, :], gs[:, qh, hl, :], m8[:, qh, hl, 2:3], None, ALU.is_ge),
                                              reads=[B_gs, B_m8], writes=[B_mbf])
                                sc.op("dve", TS(mb[:], mbf[:], 1.0e30, -1.0e30, ALU.mult, ALU.add), reads=[B_mbf], writes=[B_mb])
                                sc.op("dve", MS(mb[:, :, :, n:n + 1], 0.0), reads=[B_mb], writes=[B_mb])
                            else:
                                sc.op("dve", MS(mb[:], 0.0), writes=[B_mb])

                        def mask_b(mm, e):
                            q2 = mm % 2
                            sc.op("pe", TRS([(tpm[:, hl, qh * 128:(qh + 1) * 128], mb[:, qh, hl, :]) for qh in range(2)
                                             for hl in range(4)], ident[:]), reads=[B_mb, B_c], writes=[B_tpm])
                            sc.op("act", CPA(Qa[q2][64:96, :, e * 256:(e + 1) * 256], tpm[:]), reads=[B_tpm], writes=[B_Qa[q2]])

                        if m == 0:
                            q_load(0)
                            for e in range(2):
                                mask_a(0, e)
                                mask_b(0, e)
                        units = []
                        for hl in range(4):
                            ok = nH % 2
                            nH += 1
                            for j in range(n1 + 1):
                                units.append((hl, j, ok, nU % 2, nU % 3))
                                nU += 1

                        def emit_qk(u):
                            hl, j, ok, sk, pk = u
                            h = 4 * hg + hl
                            items = []
                            if j <= n0:
                                d0 = n0 - j
                                near = d0 <= 6
                                for kt in range(2):
                                    kk = 2 * j + kt
                                    items.append((Sp[sk][:, kt, :], KTa[0:96, hl, kk * 128:(kk + 1) * 128], Qa[qi][0:96, hl, :],
                                                  True, not near))
                                    if near:
                                        off = d0 * 256 - kt * 128 + 128
                                        items.append((Sp[sk][:, kt, :], ident[:], bC[:, hl, off:off + 512], False, True))
                                sc.op("pe", MMS(items), reads=[B_KT[(j * 256) // PW], B_Qa[qi], B_k, B_c], writes=[B_Sp[sk]])
                                if near:
                                    sc.op("act", AC(PT[pk][:], Sp[sk][:], AF.Exp), reads=[B_Sp[sk]], writes=[B_PT[pk]])
                                else:
                                    sc.op("act", AC(PT[pk][:], Sp[sk][:], AF.Exp, bias=cfar[:, h:h + 1]), reads=[B_Sp[sk], B_k],
                                          writes=[B_PT[pk]])
                            else:
                                for kt in range(2):
                                    kk = 2 * j + kt
                                    off = 128 - kt * 128
                                    items.append((Sp[sk][:, kt, 256:512], KTa[0:96, hl, kk * 128:(kk + 1) * 128],
                                                  Qa[qi][0:96, hl, 256:512], True, False))
                                    items.append((Sp[sk][:, kt, 256:512], ident[:], bC[:, hl, off:off + 256], False, True))
                                sc.op("pe", MMS(items), reads=[B_KT[(j * 256) // PW], B_Qa[qi], B_k, B_c], writes=[B_Sp[sk]])
                                sc.op("act", AC(PT[pk][:, :, 256:512], Sp[sk][:, :, 256:512], AF.Exp), reads=[B_Sp[sk]],
                                      writes=[B_PT[pk]])

                        def emit_pv(u):
                            hl, j, ok, sk, pk = u
                            items = []
                            for kt in range(2):
                                kk = 2 * j + kt
                                first = (j == 0 and kt == 0)
                                if j < n0:
                                    items.append((Op[ok][:, :], VA[:, kk, hl, :], PT[pk][:, kt, :], first, False))
                                elif j == n0 and kt == 0:
                                    items.append((Op[ok][:, :], VA[:, kk, hl, :], PT[pk][:, kt, :], first, False))
                                elif j == n0:
                                    items.append((Op[ok][:, 0:256], VA[:, kk, hl, :], PT[pk][:, kt, 0:256], False, True))
                                    items.append((Op[ok][:, 256:512], VA[:, kk, hl, :], PT[pk][:, kt, 256:512], False, False))
                                else:
                                    items.append((Op[ok][:, 256:512], VA[:, kk, hl, :], PT[pk][:, kt, 256:512], False, kt == 1))
                            sc.op("pe", MMS(items), reads=[B_VA[(j * 256) // PW], B_PT[pk]], writes=[B_Op[ok]])
                            if j == n1:
                                h = 4 * hg + hl
                                half = h % 2
                                ri = ok
                                sc.op("act", AC(rcs[ri][64:128, :], Op[ok][64:128, :], AF.Ln), reads=[B_Op[ok]], writes=[B_rcs[ri]])
                                sc.op("act", AC(rcs[ri][64:128, :], rcs[ri][64:128, :], AF.Exp, scale=-1.0), reads=[B_rcs[ri]],
                                      writes=[B_rcs[ri]])
                                sc.op("dve", TT(ocb[oi][half * 64:(half + 1) * 64, hl // 2, :], Op[ok][0:64, :], rcs[ri][64:128, :],
                                                ALU.mult), reads=[B_Op[ok], B_rcs[ri]], writes=[B_ocb[oi]])

                        LU = len(units)
                        sched_ = {}
                        if m + 1 < NCH // 2:
                            if LU >= 24:
                                sched_ = {2: [lambda: q_load(m + 1)], LU // 4: [lambda: mask_a(m + 1, 0)],
                                          LU // 2: [lambda: mask_b(m + 1, 0), lambda: mask_a(m + 1, 1)],
                                          (3 * LU) // 4: [lambda: mask_b(m + 1, 1)]}
                            else:
                                sched_ = {LU - 1: [lambda: q_load(m + 1), lambda: mask_a(m + 1, 0), lambda: mask_b(m + 1, 0),
                                                   lambda: mask_a(m + 1, 1), lambda: mask_b(m + 1, 1)]}
                        emit_qk(units[0])
                        for ui in range(1, LU):
                            emit_qk(units[ui])
                            emit_pv(units[ui - 1])
                            for f_ in sched_.get(ui, []):
                                f_()
                        emit_pv(units[-1])
                        for f_ in sched_.get(LU, []):
                            f_()
                        sc.dma("pool", ocT_d[2 * hg:2 * hg + 2, :, n0 * 256:(n0 + 2) * 256].rearrange("c p s -> p c s"), ocb[oi][:],
                               reads=[B_ocb[oi]])
                    sc.barrier()
                    sc.flush()

            with ExitStack() as st:
                wG = sb(st, "wG", [128, 8, 3072], BF16)
                wBr = sb(st, "wBr", [128, 16, 1024], BF16)
                wO = sb(st, "wO", [128, 8, 1024], BF16)
                B_wG, B_wBr, B_wO = WB(), WB(), WB()
                for q4 in range(4):
                    for br in range(3):
                        load_weights(wG, B_wG, win_d[l][:, 4352 + br * 1024 + q4 * 256:4352 + br * 1024 + (q4 + 1) * 256],
                                     br * 1024 + q4 * 256, 256)
                    load_weights(wBr, B_wBr, wbr_d[l][:, q4 * 256:(q4 + 1) * 256], q4 * 256, 256)
                load_weights(wO, B_wO, wout_d[l], 0, 1024)
                gm = sb(st, "gm", [128, D], F32)
                B_gm = Buf()
                sc.dma("sp", gm[:], mod_d[l][:, 2048:3072], writes=[B_gm])
                hT, B_hT = ring(st, "hTc", [128, 8, 512], BF16)
                oa = sb(st, "oa", [128, 4, 512], BF16)
                obt = sb(st, "obt", [128, 8, 512], BF16)
                oc = sb(st, "oc", [128, 4, 512], BF16)
                B_oa, B_ob, B_oc = Buf(), Buf(), Buf()
                gsg, B_gsg = ring(st, "gsg", [128, 3, 512], F32)
                m1, B_m1 = ring(st, "m1", [128, 3, 512], F32)
                mg = sb(st, "mg", [128, 8, 512], BF16)
                B_mg = Buf()
                xt, B_xt = ring(st, "xtc", [128, D], F32)
                xo, B_xo = ring(st, "xoc", [128, D], F32)
                gps, B_gps = ring(st, "gps", [128, 512], F32, n=3, ps=True)
                pps, B_pps = ring(st, "pps", [128, 512], F32, n=3, ps=True)
                dps, B_dps = ring(st, "dps", [128, 512], F32, n=2, ps=True)
                nj = 0
                nd_ = 0
                for t in range(NT):
                    hi = t % 2
                    ts = slice(t * 512, (t + 1) * 512)
                    sc.dma("sp", hT[hi][:], hT_d[:, :, ts].rearrange("kc p s -> p kc s"), writes=[B_hT[hi]])
                    sc.dma("sp", oa[:], oaT_d[:, :, ts].rearrange("c p s -> p c s"), writes=[B_oa])
                    sc.dma("sp", obt[:], obT_d[:, :, ts].rearrange("c p s -> p c s"), writes=[B_ob])
                    sc.dma("sp", oc[:], ocT_d[:, :, ts].rearrange("c p s -> p c s"), writes=[B_oc])
                    for j in range(8):
                        gi = nj % 2
                        nj += 1
                        for br in range(3):
                            c0 = br * 1024 + j * 128
                            sc.op("pe", MMC(gps[br][:], [(wG[:, kc, c0:c0 + 128], hT[hi][:, kc, :]) for kc in range(8)]),
                                  reads=B_wG.cover(c0, c0 + 128) + [B_hT[hi]], writes=[B_gps[br]])
                            sc.op("act", AC(gsg[gi][:, br, :], gps[br][:], AF.Sigmoid), reads=[B_gps[br]], writes=[B_gsg[gi]])
                        srcs = ((oa, B_oa, 0, 4), (obt, B_ob, 4, 8), (oc, B_oc, 12, 4))
                        for br, (src, B_src, k0, nk) in enumerate(srcs):
                            sc.op("pe", MMC(pps[br][:], [(wBr[:, k0 + kc, j * 128:(j + 1) * 128], src[:, kc, :])
                                                         for kc in range(nk)]), reads=B_wBr.cover(j * 128, (j + 1) * 128) + [B_src],
                                  writes=[B_pps[br]])
                            sc.op("dve", TT(m1[gi][:, br, :], pps[br][:], gsg[gi][:, br, :], ALU.mult),
                                  reads=[B_pps[br], B_gsg[gi]], writes=[B_m1[gi]])
                        sc.op("pool", TT(m1[gi][:, 0, :], m1[gi][:, 0, :], m1[gi][:, 1, :], ALU.add), reads=[B_m1[gi]],
                              writes=[B_m1[gi]])
                        sc.op("pool", TT(mg[:, j, :], m1[gi][:, 0, :], m1[gi][:, 2, :], ALU.add), reads=[B_m1[gi]],
                              writes=[B_mg])
                    for sub in range(4):
                        xi = sub % 2
                        r0 = t * 512 + sub * 128
                        sc.dma("sp", xt[xi][:], x_l[r0:r0 + 128, :], writes=[B_xt[xi]])
                        for hf in range(2):
                            di = nd_ % 2
                            nd_ += 1
                            fs = slice(hf * 512, (hf + 1) * 512)
                            sc.op("pe", MMC(dps[di][:], [(mg[:, kc, sub * 128:(sub + 1) * 128], wO[:, kc, fs])
                                                         for kc in range(8)]), reads=[B_mg] + B_wO.cover(hf * 512, (hf + 1) * 512),
                                  writes=[B_dps[di]])
                            sc.op("dve", TT(xo[xi][:, fs], dps[di][:], gm[:, fs], ALU.mult), reads=[B_dps[di], B_gm],
                                  writes=[B_xo[xi]])
                        sc.op("pool", TT(xo[xi][:], xo[xi][:], xt[xi][:], ALU.add), reads=[B_xo[xi], B_xt[xi]],
                              writes=[B_xo[xi]])
                        sc.dma("pool", xa_d[r0:r0 + 128, :], xo[xi][:], reads=[B_xo[xi]])
                sc.barrier()
                sc.flush()

            for hf_ in range(2):
                with ExitStack() as st:
                    wU = sb(st, "wU", [128, 8, 2816], BF16)
                    wD = sb(st, "wD", [128, 11, 1024], BF16)
                    B_wU, B_wD = WB(), WB()
                    j0 = 11 * hf_
                    for q_ in range(0, 1408, 256):
                        w_ = min(256, 1408 - q_)
                        load_weights(wU, B_wU, wup_d[l][:, j0 * 128 + q_:j0 * 128 + q_ + w_], q_, w_)
                        load_weights(wU, B_wU, wup_d[l][:, (22 + j0) * 128 + q_:(22 + j0) * 128 + q_ + w_], 1408 + q_, w_)
                    load_weights(wD, B_wD, wdn_d[l][j0 * 128:(j0 + 11) * 128, :], 0, 1024)
                    modf = sb(st, "modf", [128, 3072], F32)
                    B_mod = Buf()
                    sc.dma("sp", modf[:], mod_d[l][:, 3072:6144], writes=[B_mod])
                    pp = sb(st, "pp3", [128, NPP], F32)
                    B_pp = Buf()
                    sc.dma("sp", pp[:], pp_d[l], writes=[B_pp])
                    xt, B_xt = ring(st, "xtd", [128, D], F32)
                    xn, B_xn = ring(st, "xnd", [128, D], F32)
                    junk = sb(st, "junkd", [128, D], F32)
                    ss = sb(st, "ssd", [128, 1], F32)
                    tmp = sb(st, "tmpd", [128, D], F32)
                    hb = sb(st, "hbd", [128, D], BF16)
                    hT, B_hT = ring(st, "hTd", [128, 8, 512], BF16)
                    B_ss, B_tmp, B_hb = Buf(), Buf(), Buf()
                    halo = sb(st, "halod", [128, 22, 2], F32)
                    B_halo = Buf()
                    sc.op("pool", MS(halo[:], 0.0), writes=[B_halo])
                    us, _ = ring(st, "us", [128, 2, 514], F32, n=3)
                    yy, _ = ring(st, "yy", [128, 2, 512], F32, n=3)
                    sg, B_sg = ring(st, "sg", [128, 512], F32, n=3)
                    B_us = [[Buf(), Buf()] for _ in range(3)]
                    B_yy = [[Buf(), Buf()] for _ in range(3)]
                    aT, B_aT = ring(st, "aT", [128, 11, 512], BF16)
                    xo, B_xo = ring(st, "xod", [128, D], F32)
                    tp = psb(st, "tpd", [128, 8, 128], BF16)
                    B_tp = Buf()
                    ug, B_ug = ring(st, "ug", [128, 2, 512], F32, ps=True)
                    dps, B_dps = ring(st, "dpd", [128, 512], F32, ps=True)
                    xsrc = xa_d if hf_ == 0 else xb_d
                    xdst = xb_d if hf_ == 0 else out_d

                    def nrm(tt, sub, q=None):
                        if tt >= NT:
                            return
                        h2 = tt % 2
                        if hf_ == 0:
                            norm_sub(tt, sub, xa_d, modf[:, 1024:2048], modf[:, 0:1024], B_mod, xn, B_xn, junk, ss, B_ss, tmp, B_tmp,
                                     hb, B_hb, tp, B_tp, hT[h2], B_hT[h2], q=q)
                            if sub == 3 and q in (None, 3):
                                sc.dma("pool", hT_d[:, :, tt * 512:(tt + 1) * 512].rearrange("kc p s -> p kc s"), hT[h2][:],
                                       reads=[B_hT[h2]])
                        elif sub == 0 and q in (None, 0):
                            sc.dma("sp", hT[h2][:], hT_d[:, :, tt * 512:(tt + 1) * 512].rearrange("kc p s -> p kc s"),
                                   writes=[B_hT[h2]])

                    def stageA(t, jj, n):
                        hi, k, k3 = t % 2, n % 2, n % 3
                        for v in range(2):
                            c0 = (v * 11 + jj) * 128
                            sc.op("pe", MMC(ug[k][:, v, :], [(wU[:, kc, c0:c0 + 128], hT[hi][:, kc, :]) for kc in range(8)]),
                                  reads=B_wU.cover(c0, c0 + 128) + [B_hT[hi]], writes=[B_ug[k]])
                        sc.op("act", CPA(us[k3][:, :, 2:514], ug[k][:]), reads=[B_ug[k]], writes=B_us[k3])
                        sc.op("pool", CPV(us[k3][:, :, 0:2], halo[:, 2 * jj:2 * jj + 2, :]), reads=[B_halo], writes=B_us[k3])
                        sc.op("pool", CPV(halo[:, 2 * jj:2 * jj + 2, :], us[k3][:, :, 512:514]), reads=B_us[k3], writes=[B_halo])
                        for v in range(2):
                            ch = v * 22 + j0 + jj
                            sc.op("act", AC(yy[k3][:, v, :], us[k3][:, v, 0:512], AF.Identity,
                                            scale=pp[:, 68 + 3 * ch:69 + 3 * ch], bias=pp[:, 200 + ch:201 + ch]),
                                  reads=[B_us[k3][v], B_pp], writes=[B_yy[k3][v]])

                    def stageB(t, jj, n):
                        k3 = n % 3
                        for v in range(2):
                            ch = v * 22 + j0 + jj
                            for kk in (1, 2):
                                sc.op("dve", STT(yy[k3][:, v, :], us[k3][:, v, kk:kk + 512],
                                                 pp[:, 68 + 3 * ch + kk:69 + 3 * ch + kk], yy[k3][:, v, :], ALU.mult, ALU.add),
                                      reads=[B_us[k3][v], B_pp, B_yy[k3][v]], writes=[B_yy[k3][v]])

                    def stageC(t, jj, n):
                        k3 = n % 3
                        ai = t % 2
                        sc.op("act", AC(sg[k3][:], yy[k3][:, 0, :], AF.Silu), reads=[B_yy[k3][0]], writes=[B_sg[k3]])
                        sc.op("pool", TT(aT[ai][:, jj, :], sg[k3][:], yy[k3][:, 1, :], ALU.mult), reads=[B_sg[k3], B_yy[k3][1]],
                              writes=[B_aT[ai]])

                    ndp = [0]

                    def stageD(t, g):
                        sub, hf = g // 2, g % 2
                        xi = sub % 2
                        ai = t % 2
                        r0 = t * 512 + sub * 128
                        if hf == 0:
                            sc.dma("sp", xt[xi][:], xsrc[r0:r0 + 128, :], writes=[B_xt[xi]])
                        di = ndp[0] % 2
                        ndp[0] += 1
                        fs = slice(hf * 512, (hf + 1) * 512)
                        sc.op("pe", MMC(dps[di][:], [(aT[ai][:, kc, sub * 128:(sub + 1) * 128], wD[:, kc, fs]) for kc in range(11)]),
                              reads=[B_aT[ai]] + B_wD.cover(hf * 512, (hf + 1) * 512), writes=[B_dps[di]])
                        sc.op("dve", TT(xo[xi][:, fs], dps[di][:], modf[:, 2048 + hf * 512:2048 + (hf + 1) * 512], ALU.mult),
                              reads=[B_dps[di], B_mod], writes=[B_xo[xi]])
                        if hf == 1:
                            sc.op("pool", TT(xo[xi][:], xo[xi][:], xt[xi][:], ALU.add), reads=[B_xo[xi], B_xt[xi]],
                                  writes=[B_xo[xi]])
                            sc.dma("pool", xdst[r0:r0 + 128, :], xo[xi][:], reads=[B_xo[xi]])

                    units = [(t, jj) for t in range(NT) for jj in range(11)]
                    for sub in range(4):
                        nrm(0, sub)
                    pending = []
                    stageA(units[0][0], units[0][1], 0)
                    for n, (t, jj) in enumerate(units):
                        for s_ in range(4):
                            q_ = jj - 2 * s_
                            if 0 <= q_ <= 3:
                                nrm(t + 1, s_, q_)
                        if n + 1 < len(units):
                            stageA(units[n + 1][0], units[n + 1][1], n + 1)
                        stageB(t, jj, n)
                        stageC(t, jj, n)
                        if jj == 10:
                            pending += [(t, g) for g in range(8)]
                        elif pending:
                            stageD(*pending.pop(0))
                    while pending:
                        stageD(*pending.pop(0))
                    sc.barrier()
                    sc.flush()
        print("bass instructions:", sc.nins)
    return nc


def _t5_bucket(dist):
    dist = np.maximum(dist, 0)
    me = 16
    lr = np.log(np.maximum(dist, 1).astype(np.float32) / np.float32(me)) / np.float32(math.log(2048 / me))
    large = me + (lr * np.float32(32 - me)).astype(np.int32)
    large = np.minimum(large, 31)
    return np.where(dist < me, dist, large)


def _const_inputs(rel_bias):
    rel_bias = np.asarray(rel_bias, np.float32)
    kk = np.arange(128)[:, None]
    q = np.arange(128)[None, :]
    bA = np.full((128, 8, 2, 128), NEG, np.float32)
    d_prev = 128 + q - kk
    d_own = q - kk
    for h in range(8):
        bA[:, h, 0, :] = np.where(d_prev < 128, rel_bias[_t5_bucket(d_prev), h], NEG)
        bA[:, h, 1, :] = np.where(d_own >= 0, rel_bias[_t5_bucket(d_own), h], NEG)
    m = np.arange(2176)[None, :] - 128 - kk
    bC = np.full((128, 8, 2176), NEG, np.float32)
    for h in range(8):
        bC[:, h, :] = np.where(m >= 0, rel_bias[_t5_bucket(m), 8 + h], NEG)
    cfar = np.broadcast_to(rel_bias[31, 8:16][None, :], (128, 8)).copy()
    cst = np.zeros((128, 384), np.float32)
    cst[:, 0:128] = np.eye(128, dtype=np.float32)
    for b in range(2):
        cst[b * 64:(b + 1) * 64, 128 + b * 64:128 + (b + 1) * 64] = 1.0 / 64
    cst[:, 256:384] = 1.0
    return dict(biasA=bA.reshape(128, 2048), biasC=bC.reshape(128, 8 * 2176), cfar=cfar, cst=cst)


def _layer_params(L, inp):
    f = lambda k: np.asarray(inp[k], np.float32)
    pp = np.zeros((L, 128, NPP), np.float32)
    pp[:, :, 0] = np.tile(f("qnorm_a"), (1, 2))
    pp[:, :, 1] = np.tile(f("knorm_a"), (1, 2))
    pp[:, :, 2] = np.tile(f("qnorm_c"), (1, 2))
    pp[:, :, 3] = np.tile(f("knorm_c"), (1, 2))
    cw = f("rnn_conv_w").reshape(L, 4, 8, 128)
    pp[:, :, 4:36] = cw.transpose(0, 3, 2, 1).reshape(L, 128, 32)
    pp[:, :, 36:44] = f("rnn_conv_b").reshape(L, 8, 128).transpose(0, 2, 1)
    pp[:, :, 44:52] = f("rnn_gate_a_b").reshape(L, 8, 128).transpose(0, 2, 1)
    pp[:, :, 52:60] = f("rnn_gate_x_b").reshape(L, 8, 128).transpose(0, 2, 1)
    pp[:, :, 60:68] = f("rnn_lambda").reshape(L, 8, 128).transpose(0, 2, 1)
    fw = f("ffn_conv_w").reshape(L, 3, 44, 128)
    pp[:, :, 68:200] = fw.transpose(0, 3, 2, 1).reshape(L, 128, 132)
    pp[:, :, 200:244] = f("ffn_conv_b").reshape(L, 44, 128).transpose(0, 2, 1)
    gw = np.zeros((L, 128, 2, 8, 128), np.float32)
    for a, key in enumerate(("rnn_gate_a_w", "rnn_gate_x_w")):
        w = f(key)
        for c in range(8):
            for b in range(2):
                gw[:, b * 64:(b + 1) * 64, a, c, b * 64:(b + 1) * 64] = w[:, 2 * c + b]
    sink = np.broadcast_to(np.repeat(f("sinks"), 128, axis=1).reshape(L, 1, 1024), (L, 128, 1024)).copy()
    bmod = np.broadcast_to(f("b_mod")[:, None, :], (L, 128, 6 * D)).copy()
    gains = np.stack([np.broadcast_to(f("norm_mix")[:, None, :], (L, 128, D)),
                      np.broadcast_to(f("norm_ffn")[:, None, :], (L, 128, D))], axis=1).copy()
    return dict(pp=pp, gw=gw.reshape(L, 128, 2048), sinkrow=sink, bmod=bmod, gains=gains)


def make_in_maps(inp, L, nb):
    shared = dict(_const_inputs(inp["rel_bias"]))
    shared.update(_layer_params(L, inp))
    for k in ("w_mod", "w_in", "w_branch", "w_out", "w_up", "w_down"):
        shared[k] = np.ascontiguousarray(np.asarray(inp[k], np.float32)[:L])
    maps = []
    S_ = np.asarray(inp["x"]).shape[1]
    oh = np.zeros((32, S_), np.float32)
    for j in range(S_ // 256):
        oh[j, j * 256:(j + 1) * 256] = 1.0
    shared["blk1h"] = oh
    x = np.asarray(inp["x"], np.float32)
    c = np.asarray(inp["c"], np.float32)
    for b in range(nb):
        m = dict(shared)
        m["x"] = np.ascontiguousarray(x[b])
        m["cT"] = np.ascontiguousarray(c[b].reshape(8, 128).T)
        maps.append(m)
    return maps


_NC_CACHE = {}


def kernel(**inputs):
    x = np.asarray(inputs["x"])
    B, S, _ = x.shape
    L = np.asarray(inputs["w_mod"]).shape[0]
    key = (S, L)
    if key not in _NC_CACHE:
        _NC_CACHE[key] = build(S, L)
    nc = _NC_CACHE[key]
    maps = make_in_maps(inputs, L, B)
    res = run_bass_kernel_spmd(nc, maps, core_ids=list(range(B)))
    out = np.stack([np.asarray(r["out"], np.float32) for r in res.results], axis=0)
    if DEBUG:
        DBG.update({k: np.asarray(v) for k, v in res.results[0].items()})
    return out
```
